# Optimizing a Trainium2 kernel written in Bass

```python
import jax, jax.numpy as jnp
from jax import lax
import numpy as np

D_MODEL = 1024
BATCH = 4
SEQ = 8192
DEPTH = 2

CHUNK = 64
HEAD_DIM = 64
ATT_HEADS = 6
ATT_WIDTH = ATT_HEADS * HEAD_DIM
ATT_LEFT_CHUNKS = 8
ATT_BAND = (ATT_LEFT_CHUNKS + 1) * CHUNK
REL_CLIP = 128
CONV_WIDTH = D_MODEL // 4
CONV_KERNEL = 31
RET_HEADS = 6
RET_WIDTH = RET_HEADS * HEAD_DIM
D_MIX = ATT_WIDTH + CONV_WIDTH + RET_WIDTH
D_IN = 3 * ATT_WIDTH + 2 * CONV_WIDTH + 4 * RET_WIDTH
SPLITS = (ATT_WIDTH, 2 * ATT_WIDTH, 3 * ATT_WIDTH,
          3 * ATT_WIDTH + 2 * CONV_WIDTH,
          3 * ATT_WIDTH + 2 * CONV_WIDTH + RET_WIDTH,
          3 * ATT_WIDTH + 2 * CONV_WIDTH + 2 * RET_WIDTH,
          3 * ATT_WIDTH + 2 * CONV_WIDTH + 3 * RET_WIDTH)
D_FF = 4 * D_MODEL
ROPE_BASE = 10000.0
EPS = 1e-6
NEG_INF = -1e30

kernel_name = "hybrid_chunk_attn_conv_retention_encoder"


def rms_norm(x, g):
    xf = x.astype(jnp.float32)
    y = xf * lax.rsqrt(jnp.mean(xf * xf, axis=-1, keepdims=True) + EPS)
    return (y * g.astype(jnp.float32)).astype(x.dtype)


def layer_norm(x):
    xf = x.astype(jnp.float32)
    mu = jnp.mean(xf, axis=-1, keepdims=True)
    var = jnp.mean(jnp.square(xf - mu), axis=-1, keepdims=True)
    return (xf - mu) * lax.rsqrt(var + EPS)


def chunk_attention(q, k, v, rel_bias):
    b, s, _ = q.shape
    nc = s // CHUNK
    shp = (b, nc, CHUNK, ATT_HEADS, HEAD_DIM)
    q = q.reshape(shp) * (HEAD_DIM ** -0.5)
    k = k.reshape(shp)
    v = v.reshape(shp)
    pad = ((0, 0), (ATT_LEFT_CHUNKS, 0), (0, 0), (0, 0), (0, 0))
    kp, vp = jnp.pad(k, pad), jnp.pad(v, pad)
    band_idx = jnp.arange(nc)[:, None] + jnp.arange(ATT_LEFT_CHUNKS + 1)[None, :]
    kb = kp[:, band_idx].reshape(b, nc, ATT_BAND, ATT_HEADS, HEAD_DIM)
    vb = vp[:, band_idx].reshape(b, nc, ATT_BAND, ATT_HEADS, HEAD_DIM)
    scores = jnp.einsum('bnqhd,bnkhd->bnhqk', q, kb).astype(jnp.float32)
    qpos = jnp.arange(CHUNK)
    kpos = jnp.arange(ATT_BAND)
    rel = qpos[:, None] + ATT_LEFT_CHUNKS * CHUNK - kpos[None, :]
    rel_idx = jnp.clip(rel, -REL_CLIP, REL_CLIP) + REL_CLIP
    bias = rel_bias[:, rel_idx].astype(jnp.float32)
    key_chunk = jnp.arange(nc)[:, None] + (kpos // CHUNK)[None, :] - ATT_LEFT_CHUNKS
    valid = key_chunk >= 0
    scores = jnp.where(valid[None, :, None, None, :], scores + bias[None, None], NEG_INF)
    p = jax.nn.softmax(scores, axis=-1).astype(v.dtype)
    o = jnp.einsum('bnhqk,bnkhd->bnqhd', p, vb)
    return o.reshape(b, s, ATT_WIDTH)


def conv_module(u, conv_w, conv_b, ln_g, ln_b):
    a, gate = jnp.split(u, 2, axis=-1)
    y = a * jax.nn.sigmoid(gate)
    y = jnp.pad(y, ((0, 0), (CONV_KERNEL - 1, 0), (0, 0)))
    y = lax.conv_general_dilated(y, conv_w[:, None, :], (1,), 'VALID',
                                 dimension_numbers=('NWC', 'WIO', 'NWC'),
                                 feature_group_count=CONV_WIDTH) + conv_b
    y = layer_norm(y) * ln_g.astype(jnp.float32) + ln_b.astype(jnp.float32)
    return jax.nn.silu(y).astype(u.dtype)


def rotary(x, pos):
    half = HEAD_DIM // 2
    inv = 1.0 / (ROPE_BASE ** jnp.linspace(0.0, 1.0, half, dtype=jnp.float32))
    ang = pos[:, None] * inv[None, :]
    cos = jnp.cos(ang)[None, :, None, :]
    sin = jnp.sin(ang)[None, :, None, :]
    x1 = x[..., :half].astype(jnp.float32)
    x2 = x[..., half:].astype(jnp.float32)
    out = jnp.concatenate([x1 * cos - x2 * sin, x2 * cos + x1 * sin], axis=-1)
    return out.astype(x.dtype)


def retention(q, k, v, g):
    b, s, _ = q.shape
    nc = s // CHUNK
    dt = q.dtype
    pos = jnp.arange(s, dtype=jnp.float32)
    q = rotary(q.reshape(b, s, RET_HEADS, HEAD_DIM), pos)
    k = rotary(k.reshape(b, s, RET_HEADS, HEAD_DIM), pos) * (HEAD_DIM ** -0.5)
    shp = (b, nc, CHUNK, RET_HEADS, HEAD_DIM)
    qc, kc, vc = q.reshape(shp), k.reshape(shp), v.reshape(shp)
    log_g = jnp.log1p(-(2.0 ** (-5.0 - jnp.arange(RET_HEADS, dtype=jnp.float32))))
    idx = jnp.arange(CHUNK, dtype=jnp.float32)
    diff = idx[:, None] - idx[None, :]
    intra = jnp.where(diff >= 0, jnp.exp(jnp.maximum(diff, 0.0) * log_g[:, None, None]), 0.0)
    xi = jnp.exp((idx + 1.0)[None, :] * log_g[:, None]).astype(dt)
    zeta = jnp.exp((CHUNK - 1.0 - idx)[None, :] * log_g[:, None]).astype(dt)
    gamma_c = jnp.exp(CHUNK * log_g).astype(dt)
    sc = jnp.einsum('bnihd,bnjhd->bnhij', qc, kc) * intra.astype(dt)
    y_intra = jnp.einsum('bnhij,bnjhe->bnihe', sc, vc)
    kv = jnp.einsum('bnjhd,hj,bnjhe->nbhde', kc, zeta, vc)

    def step(state, kv_n):
        return state * gamma_c[None, :, None, None] + kv_n, state

    _, state_prev = lax.scan(step, jnp.zeros_like(kv[0]), kv)
    y_cross = jnp.einsum('bnihd,hi,nbhde->bnihe', qc, xi, state_prev)
    y = (y_intra + y_cross).reshape(b, s, RET_HEADS, HEAD_DIM)
    y = layer_norm(y).reshape(b, s, RET_WIDTH)
    return (jax.nn.silu(g.astype(jnp.float32)) * y).astype(dt)


def hybrid_layer(x, g_mix_pre, g_mix_post, g_mlp_pre, g_mlp_post, w_in, rel_bias,
                 conv_w, conv_b, conv_ln_g, conv_ln_b, w_out, w_up, w_down):
    h = rms_norm(x, g_mix_pre)
    u = h @ w_in
    aq, ak, av, cu, rq, rk, rv, rg = jnp.split(u, SPLITS, axis=-1)
    y_att = chunk_attention(aq, ak, av, rel_bias)
    y_conv = conv_module(cu, conv_w, conv_b, conv_ln_g, conv_ln_b)
    y_ret = retention(rq, rk, rv, rg)
    mix = jnp.concatenate([y_att, y_conv, y_ret], axis=-1) @ w_out
    x = x + rms_norm(mix, g_mix_post)
    h = rms_norm(x, g_mlp_pre)
    f = jnp.square(jax.nn.relu(h @ w_up)) @ w_down
    return x + rms_norm(f, g_mlp_post)


def setup_inputs(seed: int = 0) -> dict:
    key = jax.random.key(seed)
    ks = jax.random.split(key, 16)
    f32 = jnp.float32

    def nrm(k, shape, scale):
        return jax.random.normal(k, shape, f32) * scale

    def gain(k, shape):
        return 1.0 + nrm(k, shape, 0.02)

    return {
        "x": nrm(ks[0], (BATCH, SEQ, D_MODEL), 1.0),
        "norm_mix_pre": gain(ks[1], (DEPTH, D_MODEL)),
        "norm_mix_post": gain(ks[2], (DEPTH, D_MODEL)),
        "norm_mlp_pre": gain(ks[3], (DEPTH, D_MODEL)),
        "norm_mlp_post": gain(ks[4], (DEPTH, D_MODEL)),
        "w_in": nrm(ks[5], (DEPTH, D_MODEL, D_IN), D_MODEL ** -0.5),
        "attn_rel_bias": nrm(ks[6], (DEPTH, ATT_HEADS, 2 * REL_CLIP + 1), 0.5),
        "conv_w": nrm(ks[7], (DEPTH, CONV_KERNEL, CONV_WIDTH), CONV_KERNEL ** -0.5),
        "conv_b": nrm(ks[8], (DEPTH, CONV_WIDTH), 0.02),
        "conv_ln_g": gain(ks[9], (DEPTH, CONV_WIDTH)),
        "conv_ln_b": nrm(ks[10], (DEPTH, CONV_WIDTH), 0.02),
        "w_out": nrm(ks[11], (DEPTH, D_MIX, D_MODEL), D_MIX ** -0.5),
        "w_up": nrm(ks[12], (DEPTH, D_MODEL, D_FF), D_MODEL ** -0.5),
        "w_down": nrm(ks[13], (DEPTH, D_FF, D_MODEL), D_FF ** -0.5),
    }


def reference(x, norm_mix_pre, norm_mix_post, norm_mlp_pre, norm_mlp_post, w_in,
              attn_rel_bias, conv_w, conv_b, conv_ln_g, conv_ln_b, w_out, w_up, w_down):
    for l in range(DEPTH):
        x = hybrid_layer(x, norm_mix_pre[l], norm_mix_post[l], norm_mlp_pre[l], norm_mlp_post[l],
                         w_in[l], attn_rel_bias[l], conv_w[l], conv_b[l], conv_ln_g[l],
                         conv_ln_b[l], w_out[l], w_up[l], w_down[l])
    return x
```

```python
import numpy as np
import ml_dtypes
import concourse.bass as bass
import concourse.mybir as mybir
from concourse.bass_utils import run_bass_kernel_spmd

F32 = mybir.dt.float32
BF16 = mybir.dt.bfloat16
AF = mybir.ActivationFunctionType
ALU = mybir.AluOpType
AX = mybir.AxisListType


class Reg:
    def __init__(self, name, t=None):
        self.name = name
        self.t = t
        self.w = None
        self.r = []
        self.dsem = None


class _Rec:
    def __init__(self):
        self.call = None

    def __getattr__(self, name):
        def f(*a, **k):
            self.call = (name, a, k)
            return None
        return f


class Prog:
    ENG = ('pe', 'dve', 'act', 'pool', 'sp')

    def __init__(self, nc):
        self.nc = nc
        self.q = {e: [] for e in self.ENG}
        self.sems = {}
        self.cnt = {}
        self.waited = {e: {} for e in self.ENG}
        self.pending = {e: False for e in self.ENG}
        for e in ('pe', 'dve', 'act', 'pool'):
            self._newsem(e)
        self.out_tokens = []
        self.nregs = 0
        self.skip = False
        self.dbg_stage = 99

    def _newsem(self, key):
        self.sems[key] = self.nc.alloc_semaphore("s_" + key)
        self.cnt[key] = 0

    def sb(self, name, shape, dtype):
        return Reg(name, self.nc.alloc_sbuf_tensor("sb_" + name, list(shape), dtype))

    def ps(self, name, shape, dtype=F32):
        return Reg(name, self.nc.alloc_psum_tensor("ps_" + name, list(shape), dtype))

    def reg(self, name):
        return Reg(name)

    def _deps(self, e, reads, writes):
        deps = []
        for r in reads:
            if r.w is not None:
                deps.append((r.w, 'raw'))
        for w in writes:
            if w.w is not None:
                deps.append((w.w, 'waw'))
            for t in w.r:
                deps.append((t, 'war'))
        waits = {}
        for (key, val, src), kind in deps:
            if src == e:
                if e == 'pe' or e == 'sp':
                    continue
                if kind != 'raw':
                    continue
            if src == 'dma':
                val = self.cnt[key]
            if self.waited[e].get(key, 0) >= val:
                continue
            waits[key] = max(waits.get(key, 0), val)
        for key, val in waits.items():
            self.waited[e][key] = val
        return [(self.sems[key], val) for key, val in waits.items()]

    def stage(self, n):
        if n > self.dbg_stage:
            self.skip = True

    def op(self, e, fn, reads=(), writes=(), sig=True):
        if self.skip:
            return None
        rec = _Rec()
        fn(rec)
        call = rec.call
        assert call is not None
        fn = lambda eng: getattr(eng, call[0])(*call[1], **call[2])
        waits = self._deps(e, reads, writes)
        if sig:
            self.cnt[e] += 1
            tok = (e, self.cnt[e], e)
            self.pending[e] = False
            incs = [(self.sems[e], 1)]
        else:
            tok = (e, self.cnt[e] + 1, e)
            self.pending[e] = True
            incs = []
        self.q[e].append((waits, fn, incs))
        for r in reads:
            r.r.append(tok)
        for w in writes:
            w.w = tok
            w.r = []
        return tok

    def dma(self, e, dst, out_ap, in_ap, reads=(), out=False, semreg=None, force=False):
        if self.skip and not out and not force:
            return None
        sr = semreg or dst or reads[0]
        if sr.dsem is None:
            sr.dsem = "d%d_%s" % (len(self.sems), sr.name)
            self._newsem(sr.dsem)
        key = sr.dsem
        writes = [dst] if dst is not None else []
        waits = self._deps(e, reads, writes)
        self.cnt[key] += 16
        tok = (key, self.cnt[key], 'dma')
        self.q[e].append((waits, lambda eng: eng.dma_start(out=out_ap, in_=in_ap), [(self.sems[key], 16)]))
        for r in reads:
            r.r.append(tok)
        for w in writes:
            w.w = tok
            w.r = []
        if out:
            self.out_tokens.append(tok)
        return tok

    def finish(self):
        for e in self.ENG:
            assert not self.pending[e] or self.dbg_stage < 99, "engine %s ends with a non-signalling op" % e
        waits = {}
        for key in self.cnt:
            if key not in ('pe', 'dve', 'act', 'pool') and self.cnt[key] > 0:
                waits[key] = self.cnt[key]
        fin = [(self.sems[k], v) for k, v in waits.items()]
        self.q['sp'].append((fin, None, []))
        nc = self.nc
        names = {'pe': 'tensor', 'dve': 'vector', 'act': 'scalar', 'pool': 'gpsimd', 'sp': 'sync'}
        with nc.Block() as block:
            for e in self.ENG:
                def body(eng, e=e):
                    for waits, fn, incs in self.q[e]:
                        for sem, val in waits:
                            eng.wait_ge(sem, val)
                        if fn is None:
                            continue
                        ins = fn(eng)
                        for sem, v in incs:
                            ins.then_inc(sem, v)
                getattr(block, names[e])(body)
        self.ninst = {e: len(self.q[e]) for e in self.ENG}


D = 1024
DIN = 3200
DFF = 4096
SEQ = 8192
BATCH = 4
DEPTH = 2
NCORE = 8
TOK = 4096
HALO = 512
T = 512
NB = 4
NTILE = TOK // T
SCALE = 0.125
EPS = 1e-6
NEG = -1e30

_A = 384
_PERM = np.concatenate([
    np.arange(0, 384), np.arange(384, 768), np.arange(1152, 1408), np.arange(1408, 1664),
    np.arange(768, 1152), np.arange(1664, 2048), np.arange(2048, 2432), np.arange(2432, 2816),
    np.arange(2816, 3200)])

SLABS = [("in_fm0", "w_in", 0, 0, 512), ("in_fm1", "w_in", 0, 512, 512), ("in_fm2", "w_in", 0, 1024, 256)]
SLABS += [("in_tm%d" % g, "w_in", 0, 1280 + 384 * g, 384) for g in range(5)]
SLABS += [("out%d" % c, "w_out", 0, 512 * c, 512) for c in range(2)]
SLABS += [("up%d" % s, "w_up", 0, 512 * s, 512) for s in range(8)]
SLABS += [("dn%d_%d" % (c, q), "w_down", 1024 * q, 512 * c, 512) for c in range(2) for q in range(4)]
SLAB_IDX = {s[0]: i for i, s in enumerate(SLABS)}


def build_prog(state_only, dbg_stage=99, dbg_tiles=None):
    nc = bass.Bass("TRN2", target_bir_lowering=False)

    def din(name, shape, dt=F32):
        return nc.dram_tensor(name, list(shape), dt, kind="ExternalInput").ap()

    xin = din("xin", [HALO + TOK, D])
    wsrc = {"w_in": din("w_in", [D, DIN])}
    if not state_only:
        wsrc["w_out"] = din("w_out", [D, D])
        wsrc["w_up"] = din("w_up", [D, DFF])
        wsrc["w_down"] = din("w_down", [DFF, D])
    gpre_d = din("gpre", [128, 8])
    cos_d = din("cos2", [TOK, 64])
    sin_d = din("sin2", [TOK, 64])
    zeta_d = din("zeta", [128, 384])
    gc_d = din("gc", [128, 3, 64])
    s0_d = din("s0", [128, 3, 64])
    ident_d = din("ident", [128, 128], BF16)
    if not state_only:
        gpre2_d = din("gpre2", [128, 8])
        gpost_d = din("gpost", [1, D])
        gpost2_d = din("gpost2", [1, D])
        abias_d = din("abias", [128, 6, 640])
        bmask_d = din("bmask", [128, 640])
        smask_d = din("smask", [4, 128, 640])
        convw_d = din("convw", [128, 2, 31])
        convp_d = din("convp", [128, 2, 3])
        rmask_d = din("rmask", [128, 6, 128])
        xi_d = din("xi", [128, 384])
        ones_d = din("ones", [128, 128], BF16)
        xout = nc.dram_tensor("xout", [TOK, D], F32, kind="ExternalOutput").ap()
    else:
        send = nc.dram_tensor("send", [128, 3, 64], F32, kind="ExternalOutput").ap()

    p = Prog(nc)
    p.dbg_stage = dbg_stage
    use_slabs = list(range(len(SLABS))) if not state_only else [SLAB_IDX["in_tm2"], SLAB_IDX["in_tm3"]]
    wb = {}
    wbreg = {}
    for i in use_slabs:
        nm, mat, r0, c0, ncol = SLABS[i]
        wb[i] = nc.dram_tensor("wb_" + nm, [1024, ncol], BF16, kind="Internal").ap()
        wbreg[i] = p.reg("wb_" + nm)

    NSLOT = 3
    wslab = [p.sb("wslab%d" % i, [128, 8, 512], BF16) for i in range(NSLOT)]
    xt = p.sb("xt", [128, NB, D], F32)
    h_tm = [p.sb("h_tm%d" % i, [128, D], BF16) for i in range(2)]
    hT_t = nc.alloc_sbuf_tensor("sb_hT", [128, 8, T], BF16)
    hTb = [Reg("hT%d" % b, hT_t) for b in range(NB)]
    stat = [p.sb("stat%d" % i, [128, 4], F32) for i in range(NB)]
    epst = p.sb("epst", [128, 1], F32)
    gpre = p.sb("gpre", [128, 8], F32)
    ident = p.sb("ident", [128, 128], BF16)
    cosr = p.sb("cosr", [128, NB, 64], F32)
    sinr = p.sb("sinr", [128, NB, 64], F32)
    zeta = p.sb("zeta", [128, 384], F32)
    gc = p.sb("gc", [128, 3, 64], F32)
    S = p.sb("S", [128, 3, 64], F32)
    Sb = p.sb("Sb", [128, 3, 64], BF16)
    Stmp = p.sb("Stmp", [128, 3, 64], F32)
    rk_tm = [p.sb("rk_tm%d" % b, [128, 384], BF16) for b in range(NB)]
    rvz = [p.sb("rvz%d" % b, [128, 384], BF16) for b in range(NB)]
    rotA = p.sb("rotA", [128, 384], F32)
    rotB = p.sb("rotB", [128, 384], F32)
    junk = p.sb("junk", [128, D], BF16)
    if not state_only:
        gpre2 = p.sb("gpre2", [128, 8], F32)
        gpost = p.sb("gpost", [128, D], F32)
        gpost2 = p.sb("gpost2", [128, D], F32)
        abias = p.sb("abias", [128, 6, 640], F32)
        bmask = p.sb("bmask", [128, 640], F32)
        smask = p.sb("smask", [128, 640], F32)
        convw = p.sb("convw", [128, 2, 31], F32)
        convp = p.sb("convp", [128, 2, 3], F32)
        rmask = p.sb("rmask", [128, 6, 128], F32)
        xi = p.sb("xi", [128, 384], F32)
        ones = p.sb("ones", [128, 128], BF16)
        aqT = p.sb("aqT", [128, 3, T], BF16)
        akT = [p.sb("akT%d" % i, [128, 3, T], BF16) for i in range(2)]
        V = [[p.sb("V%d_%d" % (i, b), [128, 384], BF16) for b in range(NB)] for i in range(2)]
        ca = [p.sb("ca%d" % g, [128, T], F32) for g in range(2)]
        sgm = p.sb("sgm", [128, T], F32)
        y_t = nc.alloc_sbuf_tensor("sb_yglu", [128, 2, 30 + T], F32)
        yreg = [Reg("y%d" % g, y_t) for g in range(2)]
        cacc = [p.sb("cacc%d" % g, [128, T], F32) for g in range(2)]
        mo = p.sb("mo", [128, D], F32)
        cmean_ap = mo.t[:, 0:T]
        crstd_ap = mo.t[:, T:2 * T]
        cmean = mo
        crstd = mo
        rq_tm = p.sb("rq_tm", [128, 384], BF16)
        rqT_t = nc.alloc_sbuf_tensor("sb_rqT", [128, 3, T], BF16)
        rkT_t = nc.alloc_sbuf_tensor("sb_rkT", [128, 3, T], BF16)
        rqT = [Reg("rqT%d" % b, rqT_t) for b in range(NB)]
        rkT = [Reg("rkT%d" % b, rkT_t) for b in range(NB)]
        rv = [p.sb("rv%d" % b, [128, 384], BF16) for b in range(NB)]
        sg = [p.sb("sg%d" % b, [128, 384], BF16) for b in range(NB)]
        s_sb = [p.sb("s_sb%d" % i, [128, 640], F32) for i in range(2)]
        P_sb = [p.sb("P_sb%d" % i, [128, 640], BF16) for i in range(2)]
        PT = [p.sb("PT%d" % i, [128, 5, 128], BF16) for i in range(2)]
        mxs = [p.sb("mxs%d" % i, [128, 2], F32) for i in range(2)]
        rsum = p.sb("rsum", [128, 8], F32)
        rinv = p.sb("rinv", [128, 8], F32)
        scm = p.sb("scm", [128, 6, 128], BF16)
        y_sb = p.sb("y_sb", [128, 6, 64], F32)
        ysq = p.sb("ysq", [128, 6, 64], F32)
        rst = p.sb("rst", [128, 4, 6], F32)
        mix_tm = [p.sb("mix_tm%d" % i, [128, D], BF16) for i in range(2)]
        mixT_t = nc.alloc_sbuf_tensor("sb_mixT", [128, 8, T], BF16)
        mixTb = [Reg("mixT%d" % b, mixT_t) for b in range(NB)]
        mixTc = Reg("mixTc", mixT_t)
        actT_t = nc.alloc_sbuf_tensor("sb_actT", [128, 32, T], BF16)
        actT = [Reg("actT%d" % s, actT_t) for s in range(8)]
        relu_t = ca

    MM = [p.ps("MM%d" % i, [128, 512], F32) for i in range(2)]
    SC = [p.ps("SC%d" % i, [128, 1024], F32) for i in range(2)]
    TBf = p.ps("TB", [128, 512], F32)
    TBp = TBf
    TBv = TBf.t[:].bitcast(BF16)
    OB = p.ps("OB", [128, 512], F32)
    mmi = [0]

    def next_mm():
        mmi[0] += 1
        return MM[mmi[0] % 2]

    def ld(reg, src, ap=None):
        p.dma('sp', reg, reg.t[:] if ap is None else ap, src)

    ld(ident, ident_d)
    ld(gpre, gpre_d)
    ld(zeta, zeta_d)
    ld(gc, gc_d)
    ld(S, s0_d)
    p.op('pool', lambda e: e.memset(epst.t[:], EPS), writes=[epst])
    p.op('dve', lambda e: e.tensor_copy(Sb.t[:], S.t[:]), reads=[S], writes=[Sb])
    if not state_only:
        ld(gpre2, gpre2_d)
        ld(gpost, gpost_d.partition_broadcast(128))
        ld(gpost2, gpost2_d.partition_broadcast(128))
        ld(abias, abias_d)
        ld(bmask, bmask_d)
        ld(convw, convw_d)
        ld(convp, convp_d)
        ld(rmask, rmask_d)
        ld(xi, xi_d)
        ld(ones, ones_d)
        p.op('dve', lambda e: e.tensor_tensor(abias.t[:], abias.t[:], bmask.t[:].unsqueeze(1).to_broadcast([128, 6, 640]), ALU.add),
             reads=[abias, bmask], writes=[abias])
        for g in range(2):
            p.op('pool', lambda e, g=g: e.memset(y_t[:, g, 0:30], 0.0), writes=[yreg[g]])

    tiles = list(range(NTILE)) if state_only else [-1] + list(range(NTILE))
    if dbg_tiles is not None:
        tiles = tiles[:dbg_tiles]

    def slabs_for_tile(ti):
        if state_only:
            return [SLAB_IDX["in_tm2"], SLAB_IDX["in_tm3"]]
        if ti < 0:
            return [SLAB_IDX["in_fm0"], SLAB_IDX["in_fm1"], SLAB_IDX["in_fm2"], SLAB_IDX["in_tm0"]]
        return list(range(len(SLABS)))

    seq = [(ti, s) for ti in tiles for s in slabs_for_tile(ti)]
    state = {"lp": 0, "up": 0, "cast": set()}

    def emit_load(n):
        ti, s = seq[n]
        nm, mat, r0, c0, ncol = SLABS[s]
        if s not in state["cast"]:
            state["cast"].add(s)
            prev = state.get('prev_cast')
            p.dma('pool', wbreg[s], wb[s][0:512, :], wsrc[mat][r0:r0 + 512, c0:c0 + ncol], reads=[prev] if prev is not None else [], force=True)
            p.dma('pool', wbreg[s], wb[s][512:1024, :], wsrc[mat][r0 + 512:r0 + 1024, c0:c0 + ncol], reads=[wbreg[s]], force=True)
            state['prev_cast'] = wbreg[s]
        slot = wslab[n % NSLOT]
        p.dma('sp', slot, slot.t[:, :, 0:ncol], wb[s].rearrange("(k p) n -> p k n", p=128), reads=[wbreg[s]], force=True)

    def use_slab(s_expected):
        n = state["up"]
        assert seq[n][1] == s_expected, (seq[n], s_expected)
        while state["lp"] <= min(n + 1, len(seq) - 1):
            emit_load(state["lp"])
            state["lp"] += 1
        state["up"] += 1
        return wslab[n % NSLOT]

    def rmsnorm_hT(gT):
        for b in range(NB):
            st = stat[b]
            ht = h_tm[b % 2]
            p.op('act', lambda e, b=b, st=st: e.activation(junk.t[:], xt.t[:, b, :], AF.Square, accum_out=st.t[:, 0:1]),
                 reads=[xt], writes=[junk, st])
            p.op('act', lambda e, st=st: e.activation(st.t[:, 1:2], st.t[:, 0:1], AF.Ln, bias=epst.t[:], scale=1.0 / D),
                 reads=[st, epst], writes=[st])
            p.op('act', lambda e, st=st: e.activation(st.t[:, 2:3], st.t[:, 1:2], AF.Exp, scale=-0.5),
                 reads=[st], writes=[st])
            p.op('dve', lambda e, b=b, st=st, ht=ht: e.tensor_scalar_mul(ht.t[:], xt.t[:, b, :], st.t[:, 2:3]),
                 reads=[xt, st], writes=[ht])
            for k in range(8):
                p.op('pe', lambda e, k=k, ht=ht: e.transpose(TBv[:, k * 128:(k + 1) * 128], ht.t[:, k * 128:(k + 1) * 128], ident.t[:]),
                     reads=[ht, ident], writes=[TBp], sig=(k == 7))
            p.op('dve', lambda e, b=b: e.tensor_tensor(hT_t[:, :, b * 128:(b + 1) * 128],
                                                       TBv.rearrange("p (k n) -> p k n", k=8),
                                                       gT.t[:].unsqueeze(2).to_broadcast([128, 8, 128]), ALU.mult),
                 reads=[TBp, gT], writes=[hTb[b]])

    def rotary(src_ps, dst, b, scale):
        s4 = src_ps.t[:, 0:384].rearrange("p (h t d) -> p h t d", h=6, t=2)
        A4 = rotA.t[:].rearrange("p (h t d) -> p h t d", h=6, t=2)
        B4 = rotB.t[:].rearrange("p (h t d) -> p h t d", h=6, t=2)
        c4 = cosr.t[:, b, :].rearrange("p (t d) -> p t d", t=2).unsqueeze(1).to_broadcast([128, 6, 2, 32])
        sn = sinr.t[:, b, 0:32].unsqueeze(1).to_broadcast([128, 6, 32])
        sp_ = sinr.t[:, b, 32:64].unsqueeze(1).to_broadcast([128, 6, 32])
        p.op('dve', lambda e: e.scalar_tensor_tensor(A4, s4, scale, c4, ALU.mult, ALU.mult), reads=[src_ps, cosr], writes=[rotA])
        p.op('dve', lambda e: e.scalar_tensor_tensor(B4[:, :, 0, :], s4[:, :, 1, :], scale, sn, ALU.mult, ALU.mult), reads=[src_ps, sinr], writes=[rotB], sig=False)
        p.op('dve', lambda e: e.scalar_tensor_tensor(B4[:, :, 1, :], s4[:, :, 0, :], scale, sp_, ALU.mult, ALU.mult), reads=[src_ps, sinr], writes=[rotB])
        p.op('dve', lambda e: e.tensor_tensor(dst.t[:], rotA.t[:], rotB.t[:], ALU.add), reads=[rotA, rotB], writes=[dst])

    def proj_tm(slab, b, ncol=384):
        ps = next_mm()
        for k in range(8):
            p.op('pe', lambda e, k=k, ps=ps: e.matmul(ps.t[:, 0:ncol], hT_t[:, k, b * 128:(b + 1) * 128], slab.t[:, k, 0:ncol], start=(k == 0), stop=(k == 7)),
                 reads=[hTb[b], slab], writes=[ps], sig=(k == 7))
        return ps

    def proj_fm(slab, c, src_regs, src_t):
        ps = next_mm()
        for k in range(8):
            p.op('pe', lambda e, k=k, ps=ps: e.matmul(ps.t[:, 0:T], slab.t[:, k, c * 128:(c + 1) * 128], src_t[:, k, :], start=(k == 0), stop=(k == 7)),
                 reads=list(src_regs) + [slab], writes=[ps], sig=(k == 7))
        return ps

    def kv_update(b):
        KV = SC[1]
        for pr in range(3):
            p.op('pe', lambda e, pr=pr: e.matmul(KV.t[:, pr * 128:(pr + 1) * 128], rk_tm[b].t[:, pr * 128:(pr + 1) * 128], rvz[b].t[:, pr * 128:(pr + 1) * 128], start=True, stop=True),
                 reads=[rk_tm[b], rvz[b]], writes=[KV], sig=(pr == 2))
        kv3 = KV.t[:, 0:384].rearrange("p (a n) -> p a n", a=3)
        p.op('dve', lambda e: e.tensor_tensor(Stmp.t[:], S.t[:], gc.t[:], ALU.mult), reads=[S, gc], writes=[Stmp])
        p.op('dve', lambda e: e.tensor_tensor(S.t[0:64, :, :], Stmp.t[0:64, :, :], kv3[0:64, :, 0:64], ALU.add), reads=[Stmp, KV], writes=[S], sig=False)
        p.op('dve', lambda e: e.tensor_tensor(S.t[64:128, :, :], Stmp.t[64:128, :, :], kv3[64:128, :, 64:128], ALU.add), reads=[Stmp, KV], writes=[S])
        p.op('dve', lambda e: e.tensor_copy(Sb.t[:], S.t[:]), reads=[S], writes=[Sb])

    for ti in tiles:
        halo = ti < 0
        p.skip = False
        row0 = 0 if halo else HALO + ti * T
        pp = (ti + 1) % 2
        p.dma('sp', xt, xt.t[:], xin[row0:row0 + T, :].rearrange("(b p) d -> p b d", p=128))
        if not halo:
            p.dma('sp', cosr, cosr.t[:], cos_d[ti * T:(ti + 1) * T, :].rearrange("(b p) d -> p b d", p=128))
            p.dma('sp', sinr, sinr.t[:], sin_d[ti * T:(ti + 1) * T, :].rearrange("(b p) d -> p b d", p=128))
        rmsnorm_hT(gpre)

        if state_only:
            if dbg_stage < 2:
                continue
            slab = use_slab(SLAB_IDX["in_tm2"])
            for b in range(NB):
                ps = proj_tm(slab, b)
                if dbg_stage >= 3:
                    rotary(ps, rk_tm[b], b, SCALE)
            slab = use_slab(SLAB_IDX["in_tm3"])
            for b in range(NB):
                ps = proj_tm(slab, b)
                p.op('dve', lambda e, b=b, ps=ps: e.tensor_tensor(rvz[b].t[:], ps.t[:, 0:384], zeta.t[:], ALU.mult), reads=[ps, zeta], writes=[rvz[b]])
            if dbg_stage >= 4:
                for b in range(NB):
                    kv_update(b)
            continue

        p.stage(2)
        for si, nm in enumerate(["in_fm0", "in_fm1", "in_fm2"]):
            slab = use_slab(SLAB_IDX[nm])
            for cc in range(4 if si < 2 else 2):
                c = si * 4 + cc
                if halo and c < 3:
                    continue
                ps = proj_fm(slab, cc, hTb, hT_t)
                if c < 3:
                    p.op('act', lambda e, c=c, ps=ps: e.activation(aqT.t[:, c, :], ps.t[:, 0:T], AF.Copy), reads=[ps], writes=[aqT])
                elif c < 6:
                    p.op('act', lambda e, c=c, ps=ps: e.activation(akT[pp].t[:, c - 3, :], ps.t[:, 0:T], AF.Copy), reads=[ps], writes=[akT[pp]])
                elif c < 8:
                    g = c - 6
                    p.op('act', lambda e, g=g, ps=ps: e.activation(ca[g].t[:], ps.t[:, 0:T], AF.Copy), reads=[ps], writes=[ca[g]])
                else:
                    g = c - 8
                    p.op('act', lambda e, ps=ps: e.activation(sgm.t[:], ps.t[:, 0:T], AF.Sigmoid), reads=[ps], writes=[sgm])
                    p.op('pool', lambda e, g=g: e.tensor_tensor(y_t[:, g, 30:30 + T], ca[g].t[:], sgm.t[:], ALU.mult), reads=[ca[g], sgm], writes=[yreg[g]])

        p.stage(3)
        slab = use_slab(SLAB_IDX["in_tm0"])
        for b in range(NB):
            ps = proj_tm(slab, b)
            p.op('act', lambda e, b=b, ps=ps: e.activation(V[pp][b].t[:], ps.t[:, 0:384], AF.Copy), reads=[ps], writes=[V[pp][b]])
        if halo:
            for g in range(2):
                p.op('pool', lambda e, g=g: e.tensor_copy(y_t[:, g, 0:30], y_t[:, g, T:T + 30]), reads=[yreg[g]], writes=[yreg[g]])
            continue
        p.stage(3.1)
        slab = use_slab(SLAB_IDX["in_tm1"])
        for b in range(NB):
            ps = proj_tm(slab, b)
            rotary(ps, rq_tm, b, 1.0)
            for c in range(3):
                p.op('pe', lambda e, c=c: e.transpose(TBv[:, c * 128:(c + 1) * 128], rq_tm.t[:, c * 128:(c + 1) * 128], ident.t[:]), reads=[rq_tm, ident], writes=[TBp], sig=(c == 2))
            p.op('act', lambda e, b=b: e.activation(rqT_t[:, :, b * 128:(b + 1) * 128], TBv[:, 0:384].rearrange("p (c n) -> p c n", c=3), AF.Copy), reads=[TBp], writes=[rqT[b]])
        p.stage(3.2)
        slab = use_slab(SLAB_IDX["in_tm2"])
        for b in range(NB):
            ps = proj_tm(slab, b)
            rotary(ps, rk_tm[b], b, SCALE)
            for c in range(3):
                p.op('pe', lambda e, c=c, b=b: e.transpose(TBv[:, c * 128:(c + 1) * 128], rk_tm[b].t[:, c * 128:(c + 1) * 128], ident.t[:]), reads=[rk_tm[b], ident], writes=[TBp], sig=(c == 2))
            p.op('act', lambda e, b=b: e.activation(rkT_t[:, :, b * 128:(b + 1) * 128], TBv[:, 0:384].rearrange("p (c n) -> p c n", c=3), AF.Copy), reads=[TBp], writes=[rkT[b]])
        p.stage(3.3)
        slab = use_slab(SLAB_IDX["in_tm3"])
        for b in range(NB):
            ps = proj_tm(slab, b)
            p.op('act', lambda e, b=b, ps=ps: e.activation(rv[b].t[:], ps.t[:, 0:384], AF.Copy), reads=[ps], writes=[rv[b]])
            p.op('dve', lambda e, b=b: e.tensor_tensor(rvz[b].t[:], rv[b].t[:], zeta.t[:], ALU.mult), reads=[rv[b], zeta], writes=[rvz[b]])
        p.stage(3.4)
        slab = use_slab(SLAB_IDX["in_tm4"])
        for b in range(NB):
            ps = proj_tm(slab, b)
            p.op('act', lambda e, b=b, ps=ps: e.activation(sg[b].t[:], ps.t[:, 0:384], AF.Silu), reads=[ps], writes=[sg[b]])

        p.stage(4)
        for b in range(NB):
            mx = mix_tm[b % 2]
            if ti == 0:
                p.dma('sp', smask, smask.t[:], smask_d[b])
            n1 = T - 128 * b
            segs = []
            for (lo, hi, buf, off) in [(0, n1, 1 - pp, 128 * b), (n1, 640, pp, -n1)]:
                cuts = [lo] + ([512] if lo < 512 < hi else []) + [hi]
                for a, bb in zip(cuts[:-1], cuts[1:]):
                    segs.append((a, bb, buf, a + off))

            def scores(h):
                sc = SC[h % 2]
                c, r0 = h // 2, (h % 2) * 64
                for i, (a, bb, buf, src0) in enumerate(segs):
                    p.op('pe', lambda e, a=a, bb=bb, buf=buf, src0=src0: e.matmul(
                        sc.t[:, a:bb], aqT.t[r0:r0 + 64, c, b * 128:(b + 1) * 128], akT[buf].t[r0:r0 + 64, c, src0:src0 + (bb - a)], start=True, stop=True),
                        reads=[aqT, akT[buf]], writes=[sc], sig=(i == len(segs) - 1))
                s = s_sb[h % 2]
                p.op('dve', lambda e: e.scalar_tensor_tensor(s.t[:, 0:512], sc.t[:, 0:512], SCALE, abias.t[:, h, 0:512], ALU.mult, ALU.add), reads=[sc, abias], writes=[s], sig=False)
                p.op('dve', lambda e: e.scalar_tensor_tensor(s.t[:, 512:640], sc.t[:, 512:640], SCALE, abias.t[:, h, 512:640], ALU.mult, ALU.add), reads=[sc, abias], writes=[s])
                if ti == 0:
                    p.op('dve', lambda e: e.tensor_tensor(s.t[:], s.t[:], smask.t[:], ALU.add), reads=[s, smask], writes=[s])
                m = mxs[h % 2]
                p.op('dve', lambda e: e.tensor_reduce(m.t[:, 0:1], s.t[:], AX.X, ALU.max), reads=[s], writes=[m])
                p.op('dve', lambda e: e.tensor_scalar_mul(m.t[:, 1:2], m.t[:, 0:1], -1.0), reads=[m], writes=[m])
                p.op('act', lambda e: e.activation(P_sb[h % 2].t[:], s.t[:], AF.Exp, bias=m.t[:, 1:2], scale=1.0, accum_out=rsum.t[:, h:h + 1]),
                     reads=[s, m], writes=[P_sb[h % 2], rsum])

            def pv(h):
                for kb in range(5):
                    p.op('pe', lambda e, kb=kb: e.transpose(TBv[:, kb * 128:(kb + 1) * 128], P_sb[h % 2].t[:, kb * 128:(kb + 1) * 128], ident.t[:]),
                         reads=[P_sb[h % 2], ident], writes=[TBp], sig=(kb == 4))
                pt = PT[h % 2]
                p.op('dve', lambda e: e.tensor_copy(pt.t[:].rearrange("p k n -> p (k n)"), TBv[:, 0:640]), reads=[TBp], writes=[pt])
                for kb in range(5):
                    if kb < NB - b:
                        vb = V[1 - pp][b + kb]
                    else:
                        vb = V[pp][kb - (NB - b)]
                    p.op('pe', lambda e, kb=kb, vb=vb: e.matmul(OB.t[:, h * 64:(h + 1) * 64], pt.t[:, kb, :], vb.t[:, h * 64:(h + 1) * 64], start=(kb == 0), stop=(kb == 4)),
                         reads=[pt, vb], writes=[OB], sig=(kb == 4))

            for h in range(7):
                if h < 6:
                    scores(h)
                if h > 0:
                    pv(h - 1)
            p.op('dve', lambda e: e.reciprocal(rinv.t[:, 0:6], rsum.t[:, 0:6]), reads=[rsum], writes=[rinv])
            p.op('dve', lambda e, mx=mx: e.tensor_tensor(mx.t[:, 0:384].rearrange("p (h d) -> p h d", h=6), OB.t[:, 0:384].rearrange("p (h d) -> p h d", h=6),
                                                         rinv.t[:, 0:6].unsqueeze(2).to_broadcast([128, 6, 64]), ALU.mult), reads=[OB, rinv], writes=[mx])

            p.stage(5)
            for h in range(6):
                c, r0 = h // 2, (h % 2) * 64
                sc = SC[h % 2]
                p.op('pe', lambda e, h=h, c=c, r0=r0, sc=sc: e.matmul(sc.t[:, (h // 2) * 128:(h // 2 + 1) * 128], rkT_t[r0:r0 + 64, c, b * 128:(b + 1) * 128], rqT_t[r0:r0 + 64, c, b * 128:(b + 1) * 128], start=True, stop=True),
                     reads=[rkT[b], rqT[b]], writes=[sc])
            for par in range(2):
                p.op('dve', lambda e, par=par: e.tensor_tensor(scm.t[:].rearrange("p (a t) n -> p a t n", t=2)[:, :, par, :],
                                                               SC[par].t[:, 0:384].rearrange("p (a n) -> p a n", a=3),
                                                               rmask.t[:].rearrange("p (a t) n -> p a t n", t=2)[:, :, par, :], ALU.mult),
                     reads=[SC[par], rmask], writes=[scm])
            for h in range(6):
                c, r0 = h // 2, (h % 2) * 64
                yap = SC[h % 2].t[:, 512 + c * 64:512 + (c + 1) * 64]
                p.op('pe', lambda e, h=h: e.matmul(yap, scm.t[:, h, :], rv[b].t[:, h * 64:(h + 1) * 64], start=True, stop=False),
                     reads=[scm, rv[b]], writes=[SC[h % 2]], sig=False)
                p.op('pe', lambda e, h=h, c=c, r0=r0: e.matmul(yap, rqT_t[r0:r0 + 64, c, b * 128:(b + 1) * 128], Sb.t[r0:r0 + 64, c, :], start=False, stop=True),
                     reads=[rqT[b], Sb], writes=[SC[h % 2]], sig=(h >= 4))
            for par in range(2):
                p.op('dve', lambda e, par=par: e.tensor_tensor(y_sb.t[:].rearrange("p (a t) d -> p a t d", t=2)[:, :, par, :],
                                                               SC[par].t[:, 512:704].rearrange("p (a d) -> p a d", a=3),
                                                               xi.t[:].rearrange("p (a t d) -> p a t d", t=2, d=64)[:, :, par, :], ALU.mult),
                     reads=[SC[par], xi], writes=[y_sb])
            kv_update(b)
            p.op('dve', lambda e: e.tensor_reduce(rst.t[:, 0, :], y_sb.t[:], AX.X, ALU.add), reads=[y_sb], writes=[rst])
            p.op('pool', lambda e: e.tensor_tensor(ysq.t[:], y_sb.t[:], y_sb.t[:], ALU.mult), reads=[y_sb], writes=[ysq])
            p.op('dve', lambda e: e.tensor_reduce(rst.t[:, 1, :], ysq.t[:], AX.X, ALU.add), reads=[ysq], writes=[rst])
            p.op('dve', lambda e: e.tensor_scalar_mul(rst.t[:, 0, :], rst.t[:, 0, :], 1.0 / 64), reads=[rst], writes=[rst])
            p.op('dve', lambda e: e.tensor_tensor(rst.t[:, 2, :], rst.t[:, 0, :], rst.t[:, 0, :], ALU.mult), reads=[rst], writes=[rst])
            p.op('dve', lambda e: e.scalar_tensor_tensor(rst.t[:, 1, :], rst.t[:, 1, :], 1.0 / 64, rst.t[:, 2, :], ALU.mult, ALU.subtract), reads=[rst], writes=[rst])
            p.op('act', lambda e: e.activation(rst.t[:, 2, :], rst.t[:, 1, :], AF.Ln, bias=epst.t[:], scale=1.0), reads=[rst, epst], writes=[rst])
            p.op('act', lambda e: e.activation(rst.t[:, 3, :], rst.t[:, 2, :], AF.Exp, scale=-0.5), reads=[rst], writes=[rst])
            p.op('dve', lambda e: e.tensor_tensor(y_sb.t[:], y_sb.t[:], rst.t[:, 0, :].unsqueeze(2).to_broadcast([128, 6, 64]), ALU.subtract), reads=[y_sb, rst], writes=[y_sb])
            p.op('dve', lambda e: e.tensor_tensor(y_sb.t[:], y_sb.t[:], rst.t[:, 3, :].unsqueeze(2).to_broadcast([128, 6, 64]), ALU.mult), reads=[y_sb, rst], writes=[y_sb])
            p.op('dve', lambda e, mx=mx: e.tensor_tensor(mx.t[:, 640:1024], y_sb.t[:].rearrange("p h d -> p (h d)"), sg[b].t[:], ALU.mult), reads=[y_sb, sg[b]], writes=[mx])
            for i, kc in enumerate([0, 1, 2, 5, 6, 7]):
                p.op('pe', lambda e, i=i, kc=kc, mx=mx: e.transpose(TBv[:, i * 128:(i + 1) * 128], mx.t[:, kc * 128:(kc + 1) * 128], ident.t[:]), reads=[mx, ident], writes=[TBp], sig=(i == 5))
            p.op('act', lambda e, b=b: e.activation(mixT_t[:, 0:3, b * 128:(b + 1) * 128], TBv[:, 0:384].rearrange("p (c n) -> p c n", c=3), AF.Copy), reads=[TBp], writes=[mixTb[b]], sig=False)
            p.op('act', lambda e, b=b: e.activation(mixT_t[:, 5:8, b * 128:(b + 1) * 128], TBv[:, 384:768].rearrange("p (c n) -> p c n", c=3), AF.Copy), reads=[TBp], writes=[mixTb[b]])

        p.skip = False
        p.stage(6)
        for g in range(2):
            acc = cacc[g]
            p.op('pool', lambda e, g=g, acc=acc: e.tensor_scalar(acc.t[:], y_t[:, g, 0:T], convw.t[:, g, 0:1], convp.t[:, g, 0:1], ALU.mult, ALU.add),
                 reads=[yreg[g], convw, convp], writes=[acc])
            for j in range(1, 31):
                p.op('dve', lambda e, g=g, j=j, acc=acc: e.scalar_tensor_tensor(acc.t[:], y_t[:, g, j:j + T], convw.t[:, g, j:j + 1], acc.t[:], ALU.mult, ALU.add),
                     reads=[yreg[g], convw, acc], writes=[acc])
            p.op('pool', lambda e, g=g: e.tensor_copy(y_t[:, g, 0:30], y_t[:, g, T:T + 30]), reads=[yreg[g]], writes=[yreg[g]])
            p.op('pool', lambda e, g=g, acc=acc: e.tensor_copy(P_sb[g].t[:, 0:T], acc.t[:]), reads=[acc], writes=[P_sb[g]])
            p.op('pool', lambda e, g=g, acc=acc: e.tensor_tensor(PT[g].t[:].rearrange("p k n -> p (k n)")[:, 0:T], acc.t[:], acc.t[:], ALU.mult), reads=[acc], writes=[PT[g]])
        M1, M2 = SC[0], SC[1]
        for g in range(2):
            p.op('pe', lambda e, g=g: e.matmul(M1.t[:, 0:T], ones.t[:], P_sb[g].t[:, 0:T], start=(g == 0), stop=(g == 1)), reads=[ones, P_sb[g]], writes=[M1], sig=False)
        for g in range(2):
            p.op('pe', lambda e, g=g: e.matmul(M2.t[:, 0:T], ones.t[:], PT[g].t[:].rearrange("p k n -> p (k n)")[:, 0:T], start=(g == 0), stop=(g == 1)), reads=[ones, PT[g]], writes=[M2], sig=(g == 1))
        p.op('act', lambda e: e.activation(cmean_ap, M1.t[:, 0:T], AF.Copy, scale=1.0 / 256), reads=[M1], writes=[cmean])
        p.op('dve', lambda e: e.tensor_tensor(crstd_ap, cmean_ap, cmean_ap, ALU.mult), reads=[cmean], writes=[crstd])
        p.op('dve', lambda e: e.scalar_tensor_tensor(crstd_ap, M2.t[:, 0:T], 1.0 / 256, crstd_ap, ALU.mult, ALU.subtract), reads=[M2, crstd], writes=[crstd])
        p.op('act', lambda e: e.activation(crstd_ap, crstd_ap, AF.Ln, bias=epst.t[:], scale=1.0), reads=[crstd, epst], writes=[crstd])
        p.op('act', lambda e: e.activation(crstd_ap, crstd_ap, AF.Exp, scale=-0.5), reads=[crstd], writes=[crstd])
        for g in range(2):
            acc = cacc[g]
            p.op('dve', lambda e, acc=acc: e.tensor_tensor(acc.t[:], acc.t[:], cmean_ap, ALU.subtract), reads=[acc, cmean], writes=[acc])
            p.op('dve', lambda e, acc=acc: e.tensor_tensor(acc.t[:], acc.t[:], crstd_ap, ALU.mult), reads=[acc, crstd], writes=[acc])
            p.op('act', lambda e, g=g, acc=acc: e.activation(mixT_t[:, 3 + g, :], acc.t[:], AF.Silu, scale=convp.t[:, g, 1:2], bias=convp.t[:, g, 2:3]),
                 reads=[acc, convp], writes=[mixTc])

        p.stage(7)
        if dbg_tiles is not None and ti == 0:
            dbgmix = nc.dram_tensor("dbgmix", [128, 8, T], BF16, kind="ExternalOutput").ap()
            p.dma('sp', None, dbgmix, mixT_t[:], reads=mixTb + [mixTc], out=True, semreg=mixTc)
        def post_norm_residual(b, gtab):
            st = stat[b]
            p.op('dve', lambda e, st=st: e.tensor_tensor(st.t[:, 0:1], st.t[:, 0:1], st.t[:, 1:2], ALU.add), reads=[st], writes=[st])
            p.op('act', lambda e, st=st: e.activation(st.t[:, 1:2], st.t[:, 0:1], AF.Ln, bias=epst.t[:], scale=1.0 / D), reads=[st, epst], writes=[st])
            p.op('act', lambda e, st=st: e.activation(st.t[:, 2:3], st.t[:, 1:2], AF.Exp, scale=-0.5), reads=[st], writes=[st])
            p.op('dve', lambda e, st=st: e.scalar_tensor_tensor(mo.t[:], mo.t[:], st.t[:, 2:3], gtab.t[:], ALU.mult, ALU.mult), reads=[mo, st, gtab], writes=[mo])
            p.op('pool', lambda e, b=b: e.tensor_tensor(xt.t[:, b, :], xt.t[:, b, :], mo.t[:], ALU.add), reads=[xt, mo], writes=[xt])

        slabs_o = [use_slab(SLAB_IDX["out0"]), None]
        for b in range(NB):
            for c in range(2):
                if b == 0 and c == 1:
                    slabs_o[1] = use_slab(SLAB_IDX["out1"])
                slab = slabs_o[c]
                ps = next_mm()
                for k in range(8):
                    rr = [mixTb[b], mixTc, slab]
                    p.op('pe', lambda e, k=k, ps=ps, slab=slab: e.matmul(ps.t[:, 0:512], mixT_t[:, k, b * 128:(b + 1) * 128], slab.t[:, k, 0:512], start=(k == 0), stop=(k == 7)),
                         reads=rr, writes=[ps], sig=(k == 7))
                p.op('dve', lambda e, ps=ps, c=c: e.tensor_copy(mo.t[:, c * 512:(c + 1) * 512], ps.t[:, 0:512]), reads=[ps], writes=[mo])
                p.op('act', lambda e, b=b, c=c: e.activation(junk.t[:, 0:512], mo.t[:, c * 512:(c + 1) * 512], AF.Square, accum_out=stat[b].t[:, c:c + 1]), reads=[mo], writes=[junk, stat[b]])
            post_norm_residual(b, gpost)

        p.stage(8)
        rmsnorm_hT(gpre2)
        for s in range(8):
            slab = use_slab(SLAB_IDX["up%d" % s])
            for cc in range(4):
                ps = proj_fm(slab, cc, hTb, hT_t)
                rl = relu_t[cc % 2]
                p.op('act', lambda e, ps=ps, rl=rl: e.activation(rl.t[:], ps.t[:, 0:T], AF.Relu), reads=[ps], writes=[rl])
                p.op('pool' if cc % 2 else 'dve', lambda e, rl=rl, s=s, cc=cc: e.tensor_tensor(actT_t[:, s * 4 + cc, :], rl.t[:], rl.t[:], ALU.mult), reads=[rl], writes=[actT[s]])
        accs = [(MM[0], MM[0].t[:, 0:512]), (MM[1], MM[1].t[:, 0:512]), (SC[0], SC[0].t[:, 0:512]), (SC[0], SC[0].t[:, 512:1024]),
                (SC[1], SC[1].t[:, 0:512]), (SC[1], SC[1].t[:, 512:1024]), (TBp, TBp.t[:, 0:512]), (OB, OB.t[:, 0:512])]
        for c in range(2):
            for q in range(4):
                slab = use_slab(SLAB_IDX["dn%d_%d" % (c, q)])
                for b in range(NB):
                    areg, acc_ap = accs[b * 2 + c]
                    for k in range(8):
                        kk = q * 8 + k
                        p.op('pe', lambda e: e.matmul(acc_ap, actT_t[:, kk, b * 128:(b + 1) * 128], slab.t[:, k, 0:512], start=(kk == 0), stop=(kk == 31)),
                             reads=[actT[kk // 4], slab], writes=[areg], sig=(k == 7))
        for b in range(NB):
            st = stat[b]
            for c in range(2):
                areg, acc_ap = accs[b * 2 + c]
                p.op('act', lambda e: e.activation(junk.t[:, 0:512], acc_ap, AF.Square, accum_out=st.t[:, c:c + 1]), reads=[areg], writes=[junk, st])
            p.op('dve', lambda e: e.tensor_tensor(st.t[:, 0:1], st.t[:, 0:1], st.t[:, 1:2], ALU.add), reads=[st], writes=[st])
            p.op('act', lambda e: e.activation(st.t[:, 1:2], st.t[:, 0:1], AF.Ln, bias=epst.t[:], scale=1.0 / D), reads=[st, epst], writes=[st])
            p.op('act', lambda e: e.activation(st.t[:, 2:3], st.t[:, 1:2], AF.Exp, scale=-0.5), reads=[st], writes=[st])
            for c in range(2):
                areg, acc_ap = accs[b * 2 + c]
                p.op('dve', lambda e: e.scalar_tensor_tensor(mo.t[:, c * 512:(c + 1) * 512], acc_ap, st.t[:, 2:3], gpost2.t[:, c * 512:(c + 1) * 512], ALU.mult, ALU.mult),
                     reads=[areg, st, gpost2], writes=[mo])
            p.op('pool', lambda e: e.tensor_tensor(xt.t[:, b, :], xt.t[:, b, :], mo.t[:], ALU.add), reads=[xt, mo], writes=[xt])
        p.dma('sp', None, xout[ti * T:(ti + 1) * T, :].rearrange("(b p) d -> p b d", p=128), xt.t[:], reads=[xt], out=True)

    if state_only:
        p.dma('sp', None, send, S.t[:], reads=[S], out=True)
    p.finish()
    return nc, p


def _consts():
    h = np.arange(6, dtype=np.float64)
    lg = np.log1p(-(2.0 ** (-5.0 - h)))
    j = np.arange(128, dtype=np.float64)
    rmask = np.zeros((128, 6, 128), np.float64)
    tri = (j[:, None] <= j[None, :]).astype(np.float64)
    for hh in range(6):
        rmask[:, hh, :] = np.exp(-(j[:, None] + 1.0) * lg[hh]) * tri
    xi = np.repeat(np.exp((j[:, None] + 1.0) * lg[None, :]), 64, axis=1)
    zeta = np.repeat(np.exp((127.0 - j[:, None]) * lg[None, :]), 64, axis=1)
    gc = np.zeros((128, 3, 64), np.float64)
    for a in range(3):
        for hh in range(2):
            gc[hh * 64:(hh + 1) * 64, a, :] = np.exp(128.0 * lg[2 * a + hh])
    q = np.arange(128)[:, None]
    k = np.arange(640)[None, :]
    bmask = np.where(((q < 64) & (k >= 576)) | ((q >= 64) & (k < 64)), NEG, 0.0)
    rel_idx = np.clip(q + 512 - k, -128, 128) + 128
    inv = (1.0 / (np.float32(10000.0) ** np.linspace(0.0, 1.0, 32, dtype=np.float32))).astype(np.float32)
    return dict(rmask=rmask.astype(np.float32), xi=xi.astype(np.float32), zeta=zeta.astype(np.float32),
                gc=gc.astype(np.float32), bmask=bmask.astype(np.float32), rel_idx=rel_idx, inv=inv)


def _rot_tables(half, inv):
    pos = (half * TOK + np.arange(TOK)).astype(np.float32)
    ang = (pos[:, None] * inv[None, :]).astype(np.float32).astype(np.float64)
    c, s_ = np.cos(ang), np.sin(ang)
    cos2 = np.concatenate([c, c], axis=1).astype(np.float32)
    sin2 = np.concatenate([-s_, s_], axis=1).astype(np.float32)
    return cos2, sin2


_PROGS = {}


def _prog(state_only):
    if state_only not in _PROGS:
        _PROGS[state_only] = build_prog(state_only)[0]
    return _PROGS[state_only]


def _layer_inputs(l, x, C, inputs):
    gT = lambda g: np.ascontiguousarray(g.reshape(8, 128).T)
    w_in = np.ascontiguousarray(inputs["w_in"][l][:, _PERM])
    abias = np.ascontiguousarray(np.transpose(inputs["attn_rel_bias"][l][:, C["rel_idx"]], (1, 0, 2)))
    convw = np.ascontiguousarray(np.transpose(inputs["conv_w"][l].reshape(31, 2, 128), (2, 1, 0)))
    convp = np.ascontiguousarray(np.stack([inputs["conv_b"][l].reshape(2, 128).T, inputs["conv_ln_g"][l].reshape(2, 128).T,
                                           inputs["conv_ln_b"][l].reshape(2, 128).T], axis=2))
    common = dict(w_in=w_in, gpre=gT(inputs["norm_mix_pre"][l]), zeta=C["zeta"], gc=C["gc"],
                  ident=np.eye(128, dtype=ml_dtypes.bfloat16))
    full = dict(w_out=inputs["w_out"][l], w_up=inputs["w_up"][l], w_down=inputs["w_down"][l],
                gpre2=gT(inputs["norm_mlp_pre"][l]), gpost=inputs["norm_mix_post"][l].reshape(1, D),
                gpost2=inputs["norm_mlp_post"][l].reshape(1, D), abias=abias, bmask=C["bmask"], convw=convw, convp=convp,
                rmask=C["rmask"], xi=C["xi"], ones=np.ones((128, 128), dtype=ml_dtypes.bfloat16))
    percore = []
    for c in range(NCORE):
        b, half = c // 2, c % 2
        xin = np.zeros((HALO + TOK, D), np.float32)
        if half == 1:
            xin[:HALO] = x[b, TOK - HALO:TOK]
        xin[HALO:] = x[b, half * TOK:(half + 1) * TOK]
        cos2, sin2 = _rot_tables(half, C["inv"])
        smask = np.zeros((4, 128, 640), np.float32)
        if half == 0:
            for jb in range(4):
                smask[jb, :, :128 * (4 - jb)] = NEG
        percore.append(dict(xin=xin, cos2=cos2, sin2=sin2, smask=smask))
    return common, full, percore


def run_layer(l, x, C, inputs):
    common, full, percore = _layer_inputs(l, x, C, inputs)
    zero_s = np.zeros((128, 3, 64), np.float32)
    maps = [dict(common, xin=pc["xin"], cos2=pc["cos2"], sin2=pc["sin2"], s0=zero_s) for pc in percore]
    res = run_bass_kernel_spmd(_prog(True), maps, core_ids=list(range(NCORE)))
    send = [np.asarray(r["send"]) for r in res.results]
    maps = []
    for c, pc in enumerate(percore):
        s0 = send[c - 1] if c % 2 == 1 else zero_s
        maps.append(dict(common, **full, xin=pc["xin"], cos2=pc["cos2"], sin2=pc["sin2"], smask=pc["smask"], s0=s0))
    res = run_bass_kernel_spmd(_prog(False), maps, core_ids=list(range(NCORE)))
    out = np.empty_like(x)
    for c in range(NCORE):
        out[c // 2, (c % 2) * TOK:(c % 2 + 1) * TOK] = np.asarray(res.results[c]["xout"])
    return out


def kernel(**inputs):
    inputs = {k: np.asarray(v) for k, v in inputs.items()}
    C = _consts()
    x = np.ascontiguousarray(inputs["x"], dtype=np.float32)
    for l in range(DEPTH):
        x = run_layer(l, x, C, inputs)
    return x
```
